# Optimizing a Trainium2 kernel written in Bass

```python
import jax, jax.numpy as jnp
from jax import lax
import numpy as np

D_MODEL = 1024
BATCH = 2
SEQ = 8192
DEPTH = 4
DEC_BATCH = 128
DEC_SEQ = 8
PAST_LEN = 2048
PAGE_SIZE = 128

N_MIXERS = 3
N_HEADS = 16
HEAD_DIM = D_MODEL // N_HEADS
D_FF = 128 * ((8 * D_MODEL // 3 + 127) // 128)
CF_WIDTH = 31
SC_WIDTH = 3
FFN_WIDTH = 3
Q_BLOCK = 128
SB_BIAS_INIT = -7.0
RMS_EPS = 1e-6
LN_EPS = 1e-5
N_CF = (DEPTH + 2) // 3
N_SC = (DEPTH + 1) // 3
N_SB = DEPTH // 3

kernel_name = 'hybrid_conformer_shortconv_stickbreaking_decoder_step'


def _rms(x, g):
    xf = x.astype(jnp.float32)
    y = xf * lax.rsqrt(jnp.mean(xf * xf, axis=-1, keepdims=True) + RMS_EPS)
    return (y * g.astype(jnp.float32)).astype(x.dtype)


def _ln(x, g, b):
    xf = x.astype(jnp.float32)
    xc = xf - jnp.mean(xf, axis=-1, keepdims=True)
    y = xc * lax.rsqrt(jnp.mean(xc * xc, axis=-1, keepdims=True) + LN_EPS)
    return (y * g.astype(jnp.float32) + b.astype(jnp.float32)).astype(x.dtype)


def _causal_dwconv(buf, u, w):
    width = w.shape[0]
    xp = jnp.concatenate([buf.astype(u.dtype), u], axis=1)
    y = lax.conv_general_dilated(xp, w[:, None, :].astype(u.dtype), window_strides=(1,),
                                 padding='VALID', dimension_numbers=('NWC', 'WIO', 'NWC'),
                                 feature_group_count=u.shape[-1])
    return y, xp[:, xp.shape[1] - (width - 1):]


def _conformer_conv(h, buf, w1, b1, w_dw, b_dw, ln_g, ln_b, w2, b2):
    a, g = jnp.split(h @ w1 + b1, 2, axis=-1)
    u = a * jax.nn.sigmoid(g)
    y, new_buf = _causal_dwconv(buf, u, w_dw)
    y = jax.nn.silu(_ln(y + b_dw, ln_g, ln_b))
    return y @ w2 + b2, new_buf


def _short_conv(h, buf, w_in, w_conv, w_out):
    b_gate, c_gate, xv = jnp.split(h @ w_in, 3, axis=-1)
    y, new_buf = _causal_dwconv(buf, c_gate * xv, w_conv)
    return (b_gate * y) @ w_out, new_buf


def _conv_ffn(h, buf, w_up, w_conv, b_conv, w_down):
    y, new_buf = _causal_dwconv(buf, h @ w_up, w_conv)
    a, g = jnp.split(y + b_conv, 2, axis=-1)
    return (jax.nn.silu(g) * a) @ w_down, new_buf


def _sb_core(q, k, v, bias, q_pos, k_pos):
    z = jnp.einsum('nqhd,nkhd->nhqk', q, k, preferred_element_type=jnp.float32) * (HEAD_DIM ** -0.5)
    z = z + bias.astype(jnp.float32)[None, :, None, None]
    mask = k_pos[None, :] < q_pos[:, None]
    log_1m = jnp.where(mask, jax.nn.log_sigmoid(-z), 0.0)
    later = lax.cumsum(log_1m, axis=3, reverse=True) - log_1m
    a = jnp.where(mask, jnp.exp(jax.nn.log_sigmoid(z) + later), 0.0)
    return jnp.einsum('nhqk,nkhd->nqhd', a.astype(v.dtype), v)


def _sb_prompt(q, k, v, bias):
    n, t = q.shape[0], q.shape[1]
    nb = t // Q_BLOCK
    qb = q.reshape(n, nb, Q_BLOCK, N_HEADS, HEAD_DIM).transpose(1, 0, 2, 3, 4)
    k_pos = jnp.arange(t)

    def block(args):
        qi, bi = args
        return _sb_core(qi, k, v, bias, bi * Q_BLOCK + jnp.arange(Q_BLOCK), k_pos)

    o = lax.map(block, (qb, jnp.arange(nb)))
    return o.transpose(1, 0, 2, 3, 4).reshape(n, t, N_HEADS, HEAD_DIM)


def _trunk(x, c, cf_state, sc_state, ffn_state, attend, p):
    cf_new, sc_new, ffn_new, k_new, v_new = [], [], [], [], []
    c_act = jax.nn.silu(c)
    for i in range(DEPTH):
        kind, j = i % N_MIXERS, i // N_MIXERS
        mod = (c_act @ p['w_mod'][i] + p['b_mod'][i])[:, None, :]
        sh1, sc1, g1, sh2, sc2, g2 = jnp.split(mod, 6, axis=-1)
        h = _rms(x, p['g_pre_mix'][i]) * (1.0 + sc1) + sh1
        if kind == 0:
            out, nbuf = _conformer_conv(h, cf_state[j], p['cf_w1'][j], p['cf_b1'][j], p['cf_w_dw'][j],
                                        p['cf_b_dw'][j], p['cf_ln_g'][j], p['cf_ln_b'][j],
                                        p['cf_w2'][j], p['cf_b2'][j])
            cf_new.append(nbuf)
        elif kind == 1:
            out, nbuf = _short_conv(h, sc_state[j], p['sc_w_in'][j], p['sc_w_conv'][j], p['sc_w_out'][j])
            sc_new.append(nbuf)
        else:
            n, t = h.shape[0], h.shape[1]
            q, k, v = [u.reshape(n, t, N_HEADS, HEAD_DIM)
                       for u in jnp.split(h @ p['sb_w_qkv'][j], 3, axis=-1)]
            out = attend(j, q, k, v, p['sb_bias'][j]).reshape(n, t, D_MODEL) @ p['sb_w_o'][j]
            k_new.append(k)
            v_new.append(v)
        x = x + g1 * _rms(out, p['g_post_mix'][i])
        h = _rms(x, p['g_pre_ffn'][i]) * (1.0 + sc2) + sh2
        out, nbuf = _conv_ffn(h, ffn_state[i], p['ffn_w_up'][i], p['ffn_w_conv'][i],
                              p['ffn_b_conv'][i], p['ffn_w_down'][i])
        ffn_new.append(nbuf)
        x = x + g2 * _rms(out, p['g_post_ffn'][i])
    return (x, jnp.stack(cf_new), jnp.stack(sc_new), jnp.stack(ffn_new),
            jnp.stack(k_new), jnp.stack(v_new))


def setup_inputs(seed: int = 0) -> dict:
    key = jax.random.key(seed)
    keys = jax.random.split(key, 64)
    counter = [0]

    def nrm(shape, scale=1.0):
        kk = keys[counter[0]]
        counter[0] += 1
        return jax.random.normal(kk, shape, jnp.float32) * scale

    d, f = D_MODEL, D_FF
    n_pages = PAST_LEN // PAGE_SIZE
    n_used = DEC_BATCH * n_pages
    n_pool = n_used + max(1, n_used // 4)
    inp = {}
    inp['x_prompt'] = nrm((BATCH, SEQ, d))
    inp['x_sample'] = nrm((DEC_BATCH, DEC_SEQ, d))
    inp['c_prompt'] = nrm((BATCH, d))
    inp['c_sample'] = nrm((DEC_BATCH, d))
    inp['state_cf_conv'] = nrm((N_CF, DEC_BATCH, CF_WIDTH - 1, d), 0.5)
    inp['state_sc_conv'] = nrm((N_SC, DEC_BATCH, SC_WIDTH - 1, d), 0.5)
    inp['state_ffn_conv'] = nrm((DEPTH, DEC_BATCH, FFN_WIDTH - 1, 2 * f))
    inp['cache_k'] = nrm((N_SB, n_pool, PAGE_SIZE, N_HEADS, HEAD_DIM))
    inp['cache_v'] = nrm((N_SB, n_pool, PAGE_SIZE, N_HEADS, HEAD_DIM))
    inp['page_table'] = jax.random.permutation(keys[63], n_pool)[:n_used].reshape(
        DEC_BATCH, n_pages).astype(jnp.int32)
    inp['g_pre_mix'] = 1.0 + nrm((DEPTH, d), 0.02)
    inp['g_post_mix'] = 1.0 + nrm((DEPTH, d), 0.02)
    inp['g_pre_ffn'] = 1.0 + nrm((DEPTH, d), 0.02)
    inp['g_post_ffn'] = 1.0 + nrm((DEPTH, d), 0.02)
    inp['w_mod'] = nrm((DEPTH, d, 6 * d), d ** -0.5)
    inp['b_mod'] = nrm((DEPTH, 6 * d), 0.02)
    inp['ffn_w_up'] = nrm((DEPTH, d, 2 * f), d ** -0.5)
    inp['ffn_w_conv'] = nrm((DEPTH, FFN_WIDTH, 2 * f), FFN_WIDTH ** -0.5)
    inp['ffn_b_conv'] = nrm((DEPTH, 2 * f), 0.02)
    inp['ffn_w_down'] = nrm((DEPTH, f, d), f ** -0.5)
    inp['cf_w1'] = nrm((N_CF, d, 2 * d), d ** -0.5)
    inp['cf_b1'] = nrm((N_CF, 2 * d), 0.02)
    inp['cf_w_dw'] = nrm((N_CF, CF_WIDTH, d), CF_WIDTH ** -0.5)
    inp['cf_b_dw'] = nrm((N_CF, d), 0.02)
    inp['cf_ln_g'] = 1.0 + nrm((N_CF, d), 0.02)
    inp['cf_ln_b'] = nrm((N_CF, d), 0.02)
    inp['cf_w2'] = nrm((N_CF, d, d), d ** -0.5)
    inp['cf_b2'] = nrm((N_CF, d), 0.02)
    inp['sc_w_in'] = nrm((N_SC, d, 3 * d), d ** -0.5)
    inp['sc_w_conv'] = nrm((N_SC, SC_WIDTH, d), SC_WIDTH ** -0.5)
    inp['sc_w_out'] = nrm((N_SC, d, d), d ** -0.5)
    inp['sb_w_qkv'] = nrm((N_SB, d, 3 * d), d ** -0.5)
    inp['sb_bias'] = SB_BIAS_INIT + nrm((N_SB, N_HEADS), 0.1)
    inp['sb_w_o'] = nrm((N_SB, d, d), d ** -0.5)
    return inp


def reference(x_prompt, x_sample, c_prompt, c_sample, state_cf_conv, state_sc_conv, state_ffn_conv,
              cache_k, cache_v, page_table, g_pre_mix, g_post_mix, g_pre_ffn, g_post_ffn, w_mod, b_mod,
              ffn_w_up, ffn_w_conv, ffn_b_conv, ffn_w_down, cf_w1, cf_b1, cf_w_dw, cf_b_dw, cf_ln_g,
              cf_ln_b, cf_w2, cf_b2, sc_w_in, sc_w_conv, sc_w_out, sb_w_qkv, sb_bias, sb_w_o):
    p = dict(g_pre_mix=g_pre_mix, g_post_mix=g_post_mix, g_pre_ffn=g_pre_ffn, g_post_ffn=g_post_ffn,
             w_mod=w_mod, b_mod=b_mod, ffn_w_up=ffn_w_up, ffn_w_conv=ffn_w_conv, ffn_b_conv=ffn_b_conv,
             ffn_w_down=ffn_w_down, cf_w1=cf_w1, cf_b1=cf_b1, cf_w_dw=cf_w_dw, cf_b_dw=cf_b_dw,
             cf_ln_g=cf_ln_g, cf_ln_b=cf_ln_b, cf_w2=cf_w2, cf_b2=cf_b2, sc_w_in=sc_w_in,
             sc_w_conv=sc_w_conv, sc_w_out=sc_w_out, sb_w_qkv=sb_w_qkv, sb_bias=sb_bias, sb_w_o=sb_w_o)

    b, s = x_prompt.shape[0], x_prompt.shape[1]

    def fresh(st):
        return jnp.zeros((st.shape[0], b) + st.shape[2:], x_prompt.dtype)

    y_p, cf_p, sc_p, ffn_p, k_p, v_p = _trunk(
        x_prompt, c_prompt, fresh(state_cf_conv), fresh(state_sc_conv), fresh(state_ffn_conv),
        lambda j, q, k, v, bias: _sb_prompt(q, k, v, bias), p)

    n_dec, n_pages = page_table.shape
    past = n_pages * PAGE_SIZE

    def attend_sample(j, q, k, v, bias):
        pk = cache_k[j][page_table].reshape(n_dec, past, N_HEADS, HEAD_DIM).astype(k.dtype)
        pv = cache_v[j][page_table].reshape(n_dec, past, N_HEADS, HEAD_DIM).astype(v.dtype)
        kk = jnp.concatenate([pk, k], axis=1)
        vv = jnp.concatenate([pv, v], axis=1)
        t = q.shape[1]
        return _sb_core(q, kk, vv, bias, past + jnp.arange(t), jnp.arange(past + t))

    y_s, cf_s, sc_s, ffn_s, k_s, v_s = _trunk(
        x_sample, c_sample, state_cf_conv, state_sc_conv, state_ffn_conv, attend_sample, p)

    page_shape = (k_p.shape[0], b, s // PAGE_SIZE, PAGE_SIZE, N_HEADS, HEAD_DIM)
    return (y_p, y_s, k_p.reshape(page_shape), v_p.reshape(page_shape), k_s, v_s,
            cf_p, cf_s, sc_p, sc_s, ffn_p, ffn_s)
```

```python
import contextlib
import os
import numpy as np
import concourse.bass as bass
import concourse.mybir as mybir
from concourse.bass_utils import run_bass_kernel_spmd

F32 = mybir.dt.float32
BF16 = mybir.dt.bfloat16
I32 = mybir.dt.int32
AF = mybir.ActivationFunctionType
ALU = mybir.AluOpType

D = 1024
NH = 16
HD = 64
DFF = 2816
NT = 512
HALO = 512
NEG = -30000.0


class Buf:
    __slots__ = ("name", "lw", "rd")

    def __init__(self, name=""):
        self.name = name
        self.lw = None
        self.rd = []


class Ev:
    __slots__ = ("sem", "val", "vc")

    def __init__(self, sem, val, vc):
        self.sem = sem
        self.val = val
        self.vc = vc


class Prog:
    ENGS = ("pe", "act", "dve", "pool", "sp")
    EPOCH = 30000

    def __init__(self, nc, stack, n_dma_sems=8):
        self.nc = nc
        self.ops = {e: [] for e in self.ENGS}
        self.count = {e: 0 for e in self.ENGS}
        self.clock = {e: {} for e in self.ENGS}
        self.esem = {}
        self.esems = {}
        self.ecount = {}
        self.stack = stack
        for e in ("pe", "act", "dve", "pool"):
            self.esems[e] = [stack.enter_context(nc.semaphore("es_%s0" % e))]
            self.esem[e] = self.esems[e][0]
            self.ecount[e] = 0
        self.dsems = {}
        self.dlast = {}
        self.dnext = {}
        for q in ("sp", "pool"):
            self.dsems[q] = [stack.enter_context(nc.semaphore("ds_%s%d" % (q, i)))
                             for i in range(n_dma_sems)]
            self.dlast[q] = [None] * n_dma_sems
            self.dnext[q] = 0

    def _waits(self, eng, reads, writes, extra=()):
        ck = self.clock[eng]
        waits = {}

        def need(ev):
            if ev is None:
                return
            if ck.get(id(ev.sem), 0) >= ev.val:
                return
            if eng == "pe" and any(ev.sem is x for x in self.esems["pe"]):
                return
            cur = waits.get(id(ev.sem))
            if cur is None or cur.val < ev.val:
                waits[id(ev.sem)] = ev

        for b in reads:
            need(b.lw)
        for b in writes:
            need(b.lw)
            for r in b.rd:
                need(r)
        for ev in extra:
            need(ev)
        wl = []
        for ev in waits.values():
            wl.append((ev.sem, ev.val))
            for s, v in ev.vc.items():
                if ck.get(s, 0) < v:
                    ck[s] = v
        return wl

    def _commit(self, ev, reads, writes):
        for b in reads:
            if len(b.rd) > 24:
                b.rd = b.rd[-24:]
            b.rd.append(ev)
        for b in writes:
            b.lw = ev
            b.rd = []

    def op(self, eng, fn, reads=(), writes=()):
        wl = self._waits(eng, reads, writes)
        self.count[eng] += 1
        if self.ecount[eng] >= self.EPOCH:
            self.esems[eng].append(self.stack.enter_context(
                self.nc.semaphore("es_%s%d" % (eng, len(self.esems[eng])))))
            self.esem[eng] = self.esems[eng][-1]
            self.ecount[eng] = 0
        self.ecount[eng] += 1
        n = self.ecount[eng]
        sem = self.esem[eng]
        vc = dict(self.clock[eng])
        vc[id(sem)] = n
        ev = Ev(sem, n, vc)
        self.ops[eng].append((wl, fn, sem, 1))
        self._commit(ev, reads, writes)
        return ev

    def dma(self, q, fn, reads=(), writes=()):
        i = self.dnext[q]
        self.dnext[q] = (i + 1) % len(self.dsems[q])
        sem = self.dsems[q][i]
        prev = self.dlast[q][i]
        wl = self._waits(q, reads, writes, extra=(prev,) if prev else ())
        val = (prev.val if prev else 0) + 16
        vc = dict(self.clock[q])
        vc[id(sem)] = val
        ev = Ev(sem, val, vc)
        self.dlast[q][i] = ev
        self.ops[q].append((wl, fn, sem, 16))
        self._commit(ev, reads, writes)
        return ev

    def barrier(self):
        evs = []
        for e in ("pe", "act", "dve", "pool"):
            if self.ecount[e]:
                evs.append(Ev(self.esem[e], self.ecount[e], {}))
        for q in self.dsems:
            for ev in self.dlast[q]:
                if ev is not None:
                    evs.append(ev)
        for eng in self.ENGS:
            ck = self.clock[eng]
            wl = []
            for ev in evs:
                if ck.get(id(ev.sem), 0) < ev.val:
                    wl.append((ev.sem, ev.val))
                    ck[id(ev.sem)] = ev.val
            if wl:
                self.ops[eng].append((wl, None, None, 0))

    def finish(self):
        nc = self.nc
        self.barrier()
        ops = self.ops
        if os.environ.get('KDBG_PRINT'):
            print('OPCOUNTS', self.count, {k: len(v) for k, v in ops.items()}, flush=True)

        def emit(name, e):
            for wl, fn, sem, inc in ops[name]:
                for s, v in wl:
                    e.wait_ge(s, v)
                if fn is not None:
                    fn(e).then_inc(sem, inc)

        with nc.Block() as block:
            @block.tensor
            def _(e):
                emit("pe", e)

            @block.scalar
            def _(e):
                emit("act", e)

            @block.vector
            def _(e):
                emit("dve", e)

            @block.gpsimd
            def _(e):
                emit("pool", e)

            @block.sync
            def _(e):
                emit("sp", e)


def build(SEQ, NPG, NPOOL, do_prompt=True, do_sample=True):
    OWN = SEQ // 4
    NOWN = OWN // NT
    NPRE = SEQ // NT
    QN = HALO + OWN
    NKB = SEQ // 128
    nc = bass.Bass("TRN2", target_bir_lowering=False)

    def din(name, shape, dt=F32):
        return nc.dram_tensor(name, list(shape), dt, kind="ExternalInput").ap()

    def dout(name, shape, dt=F32):
        return nc.dram_tensor(name, list(shape), dt, kind="ExternalOutput").ap()

    def dscr(name, shape, dt):
        return nc.dram_tensor(name, list(shape), dt).ap()

    I = {}
    I["xpT"] = din("xpT", [D, SEQ])
    I["xsT"] = din("xsT", [D, 128])
    I["cT"] = din("cT", [128, 8, 17])
    I["tmask"] = din("tmask", [128, NPRE + 1])
    I["kmask"] = din("kmask", [128, NKB])
    I["cf_state"] = din("cf_state", [2, 128, 8, 16, 30])
    I["sc_state"] = din("sc_state", [128, 8, 16, 2])
    I["ffn_state"] = din("ffn_state", [4, 128, 44, 16, 2])
    I["cache_k"] = din("cache_k", [NPOOL * 128, D])
    I["cache_v"] = din("cache_v", [NPOOL * 128, D])
    I["ptab"] = din("ptab", [16, NPG], I32)
    I["gvec"] = din("gvec", [128, 16, 8])
    I["w_mod"] = din("w_mod", [4 * 48, 128, 8 * 128])
    I["b_mod"] = din("b_mod", [128, 4 * 48])
    I["ffn_up"] = din("ffn_up", [4 * 44, 128, 8 * 128])
    I["ffn_wc"] = din("ffn_wc", [128, 4, 44, 3])
    I["ffn_bc"] = din("ffn_bc", [128, 4, 44])
    I["ffn_dn"] = din("ffn_dn", [4 * 8, 128, 22 * 128])
    I["cf_w1"] = din("cf_w1", [2 * 16, 128, 8 * 128])
    I["cf_b1"] = din("cf_b1", [128, 2, 16])
    I["cf_wdw"] = din("cf_wdw", [128, 2, 8, 31])
    I["cf_vec"] = din("cf_vec", [128, 2, 4, 8])
    I["cf_w2"] = din("cf_w2", [2 * 8, 128, 8 * 128])
    I["sc_win"] = din("sc_win", [24, 128, 8 * 128])
    I["sc_wc"] = din("sc_wc", [128, 8, 3])
    I["sc_wout"] = din("sc_wout", [8, 128, 8 * 128])
    I["sb_wqk"] = din("sb_wqk", [16, 128, 8 * 128])
    I["sb_wkv2"] = din("sb_wkv2", [8, 128, 4 * 512])
    I["sb_bias"] = din("sb_bias", [128, 16])
    I["sb_wo"] = din("sb_wo", [8, 128, 8 * 128])

    O = {}
    O["ypT"] = dout("ypT", [D, OWN])
    O["ysT"] = dout("ysT", [D, 128])
    O["nkp"] = dout("nkp", [OWN, D])
    O["nvp"] = dout("nvp", [OWN, D])
    O["nks"] = dout("nks", [128, D])
    O["nvs"] = dout("nvs", [128, D])
    O["cfp"] = dout("cfp", [2, 128, 8, 30])
    O["cfs"] = dout("cfs", [2, 128, 8, 16, 30])
    O["scp"] = dout("scp", [128, 8, 2])
    O["scs"] = dout("scs", [128, 8, 16, 2])
    O["ffp"] = dout("ffp", [4, 128, 44, 2])
    O["ffs"] = dout("ffs", [4, 128, 44, 16, 2])

    kT_s = dscr("kT_s", [D, SEQ], BF16)
    v_s2 = dscr("v_s2", [8, 128, NKB, 128], BF16)
    qT_s = dscr("qT_s", [D, QN], BF16)
    x_s = dscr("x_s", [D, QN], F32)
    oT_s = dscr("oT_s", [D, QN], BF16)
    B_kT, B_v, B_qT, B_x, B_oT = Buf("kT_s"), Buf("v_s"), Buf("qT_s"), Buf("x_s"), Buf("oT_s")

    with contextlib.ExitStack() as st:
        P = Prog(nc, st)

        cur = [st]

        def sb(name, shape, dt):
            return cur[0].enter_context(nc.sbuf_tensor("s_" + name, list(shape), dt))

        def mm(out, lhsT, rhs, start, stop, rd, wr):
            P.op("pe", lambda e, a=out, b=lhsT, c=rhs, s=start, t=stop:
                 e.matmul(a, lhsT=b, rhs=c, start=s, stop=t), rd, wr)

        def act(out, in_, func, rd, wr, bias=0.0, scale=1.0):
            P.op("act", lambda e, a=out, b=in_, f=func, bi=bias, sc=scale:
                 e.activation(out=a, in_=b, func=f, bias=bi, scale=sc), rd, wr)

        def tt(out, in0, in1, op, rd, wr, eng="dve"):
            P.op(eng, lambda e, a=out, b=in0, c=in1, o=op: e.tensor_tensor(out=a, in0=b, in1=c, op=o), rd, wr)

        def stt(out, in0, scalar, in1, op0, op1, rd, wr, eng="dve"):
            P.op(eng, lambda e, a=out, b=in0, s=scalar, c=in1, o0=op0, o1=op1:
                 e.scalar_tensor_tensor(out=a, in0=b, scalar=s, in1=c, op0=o0, op1=o1), rd, wr)

        def ts(out, in0, s1, s2, op0, op1, rd, wr, eng="dve"):
            if s2 is None:
                P.op(eng, lambda e, a=out, b=in0, x=s1, o0=op0:
                     e.tensor_scalar(out=a, in0=b, scalar1=x, scalar2=None, op0=o0), rd, wr)
            else:
                P.op(eng, lambda e, a=out, b=in0, x=s1, y=s2, o0=op0, o1=op1:
                     e.tensor_scalar(out=a, in0=b, scalar1=x, scalar2=y, op0=o0, op1=o1), rd, wr)

        def cp(out, in_, rd, wr, eng="dve"):
            P.op(eng, lambda e, a=out, b=in_: e.tensor_copy(out=a, in_=b), rd, wr)

        def ms(out, val, wr, eng="dve"):
            P.op(eng, lambda e, a=out, v=val: e.memset(a, v), (), wr)

        def dma(q, out, in_, rd, wr):
            P.dma(q, lambda e, a=out, b=in_: e.dma_start(out=a, in_=b), rd, wr)

        ones_bf = sb("ones_bf", [128, 128], BF16)
        negU = sb("negU", [128, 128], BF16)
        negones = sb("negones", [128, 128], BF16)
        ident = sb("ident", [128, 128], BF16)
        identf = sb("identf", [128, 128], F32)
        masks = sb("masks", [128, 4, NT], BF16)
        B_const = Buf("const")
        onesf = sb("onesf", [128, 128], F32)
        maskf = sb("maskf", [128, NT], F32)
        ms(onesf[:, :], 1.0, [B_const], eng="pool")
        ms(maskf[:, :], 0.0, [B_const], eng="pool")
        cp(ones_bf[:, :], onesf[:, :], [B_const], [B_const], eng="pool")
        ts(negones[:, :], onesf[:, :], -1.0, None, ALU.mult, None, [B_const], [B_const], eng="pool")
        P.op("pool", lambda e: e.affine_select(out=negU[:, :], in_=negones[:, :], pattern=[[-1, 128]],
                                                compare_op=ALU.is_ge, fill=0.0, base=0, channel_multiplier=1),
             [B_const], [B_const])
        P.op("pool", lambda e: e.affine_select(out=identf[:, :], in_=onesf[:, :], pattern=[[-1, 128]],
                                                compare_op=ALU.is_equal, fill=0.0, base=0, channel_multiplier=1),
             [B_const], [B_const])
        cp(ident[:, :], identf[:, :], [B_const], [B_const], eng="pool")
        for m in range(4):
            P.op("pool", lambda e, m=m: e.affine_select(out=masks[:, m, :], in_=maskf[:, :], pattern=[[1, NT]],
                                                         compare_op=ALU.is_ge, fill=NEG, base=-128 * m - 1,
                                                         channel_multiplier=-1),
                 [B_const], [B_const])

        B_par = Buf("par")
        gvec = sb("gvec", [128, 16, 8], F32)
        bmod = sb("bmod", [128, 4 * 48], F32)
        ffn_wc = sb("ffn_wc", [128, 4, 44, 3], F32)
        ffn_bc = sb("ffn_bc", [128, 4, 44], F32)
        cf_b1 = sb("cf_b1", [128, 2, 16], F32)
        cf_wdw = sb("cf_wdw", [128, 2, 8, 31], F32)
        cf_vec = sb("cf_vec", [128, 2, 4, 8], F32)
        sc_wc = sb("sc_wc", [128, 8, 3], F32)
        sbb = sb("sbb", [128, 16], F32)
        tmask = sb("tmask", [128, NPRE + 1], F32)
        kmask = sb("kmask", [128, NKB], F32)
        cTt = sb("cTt", [128, 8, 17], F32)
        for t_, src in ((gvec, "gvec"), (bmod, "b_mod"), (ffn_wc, "ffn_wc"), (ffn_bc, "ffn_bc"),
                        (cf_b1, "cf_b1"), (cf_wdw, "cf_wdw"), (cf_vec, "cf_vec"), (sc_wc, "sc_wc"),
                        (sbb, "sb_bias"), (tmask, "tmask"), (kmask, "kmask"), (cTt, "cT")):
            dma("sp", t_[:], I[src][:], [], [B_par])

        banks = [st.enter_context(nc.psum_tensor("bank%d" % i, [128, NT], F32)) for i in range(8)]
        bbufs = [Buf("bank%d" % i) for i in range(8)]
        bstate = [0]

        def nb():
            i = bstate[0]
            bstate[0] = (i + 1) % 7
            return banks[i], bbufs[i]

        NSLOT = 3
        wslots = [sb("wslot%d" % i, [128, 22 * 128], BF16) for i in range(NSLOT)]
        wbufs = [Buf("wslot%d" % i) for i in range(NSLOT)]
        wstate = [0]

        def wload(src, width):
            i = wstate[0]
            wstate[0] = (i + 1) % NSLOT
            dma("pool", wslots[i][:, 0:width], src, [], [wbufs[i]])
            return wslots[i], wbufs[i]

        cact = sb("cact", [128, 8, 17], BF16)
        B_cact = Buf("cact")
        act(cact[:], cTt[:], AF.Silu, [B_par], [B_cact])
        modT = sb("modT", [128, 4, 48, 17], F32)
        B_mod = Buf("mod")
        for l in range(4):
            for m in range(48):
                wt, wb = wload(I["w_mod"][l * 48 + m], 1024)
                ps, pb = nb()
                for kc in range(8):
                    mm(ps[:, 0:17], wt[:, kc * 128:(kc + 1) * 128], cact[:, kc, :], kc == 0, kc == 7,
                       [wb, B_cact], [pb])
                act(modT[:, l, m, :], ps[:, 0:17], AF.Identity, [pb, B_par], [B_mod],
                    bias=bmod[:, l * 48 + m:l * 48 + m + 1])
        mder = sb("mder", [128, 4, 4, 8, 17], F32)
        for l in range(4):
            for c in range(8):
                ts(mder[:, l, 0, c, :], modT[:, l, 8 + c, :], 1.0, gvec[:, 0 * 4 + l, c:c + 1], ALU.add, ALU.mult,
                   [B_mod, B_par], [B_mod])
                ts(mder[:, l, 1, c, :], modT[:, l, 16 + c, :], gvec[:, 1 * 4 + l, c:c + 1], None, ALU.mult, None,
                   [B_mod, B_par], [B_mod])
                ts(mder[:, l, 2, c, :], modT[:, l, 32 + c, :], 1.0, gvec[:, 2 * 4 + l, c:c + 1], ALU.add, ALU.mult,
                   [B_mod, B_par], [B_mod])
                ts(mder[:, l, 3, c, :], modT[:, l, 40 + c, :], gvec[:, 3 * 4 + l, c:c + 1], None, ALU.mult, None,
                   [B_mod, B_par], [B_mod])

        class Job:
            pass

        def make_job(name, S, T, col0):
            J = Job()
            J.name, J.S, J.T, J.N, J.col0 = name, S, T, S * T, col0
            N = J.N
            J.x = sb(name + "_x", [128, 8, N], F32)
            J.Bx = [Buf(name + "_x%d" % c) for c in range(8)]
            J.h = sb(name + "_h", [128, 8, N], BF16)
            J.Bh = [Buf(name + "_h%d" % c) for c in range(8)]
            J.o = sb(name + "_o", [128, 8, N], F32)
            J.Bo = [Buf(name + "_o%d" % c) for c in range(8)]
            J.sq, J.Bsq = J.h, J.Bh
            J.tmp = sb(name + "_tmp", [128, 8, N], F32)
            J.Btmp = [Buf(name + "_tmp%d" % c) for c in range(8)]
            J.rs = [sb(name + "_rs%d" % i, [128, N], F32) for i in range(2)]
            J.Brs = [Buf(name + "_rs%d" % i) for i in range(2)]
            J.rsi = 0
            J.r1 = sb(name + "_r1", [128, N], F32)
            J.Br1 = Buf()
            wu = 8 * S * (30 + T)
            wcx = 8 * S * (2 + T)
            J.mix = sb(name + "_mix", [128, max(wu, wcx + 8 * N)], F32)
            J.u = J.mix[:, 0:wu].rearrange("p (c s t) -> p c s t", c=8, s=S)
            J.Bu = [Buf(name + "_u%d" % c) for c in range(8)]
            J.cx = J.mix[:, 0:wcx].rearrange("p (c s t) -> p c s t", c=8, s=S)
            J.Bcx = [Buf(name + "_cx%d" % c) for c in range(8)]
            J.bg = J.mix[:, wcx:wcx + 8 * N].rearrange("p (c n) -> p c n", c=8)
            J.Bbg = [Buf(name + "_bg%d" % c) for c in range(8)]
            J.sig = sb(name + "_sig", [128, 2, N], F32)
            J.Bsig = [Buf(), Buf()]
            J.up = sb(name + "_up", [128, 2, 2, S, 2 + T], F32)
            J.Bup = [[Buf(), Buf()], [Buf(), Buf()]]
            J.fa = sb(name + "_fa", [128, 2, 2, N], F32)
            J.Bfa = [[Buf(), Buf()], [Buf(), Buf()]]
            J.hid = sb(name + "_hid", [128, 22, N], BF16)
            J.Bhid = [Buf(name + "_hid%d" % c) for c in range(22)]
            J.yb, J.Byb = J.hid, J.Bhid
            J.fft = sb(name + "_fft", [128, 4, 44, S, 2], F32)
            J.Bfft = [Buf(name + "_fft%d" % l) for l in range(4)]
            return J

        def v3(ap2, S, T):
            return ap2.rearrange("p (s t) -> p s t", s=S)

        def rstd_of(J, src, Bsrc, C, eps, mean_too=False):
            N = J.N
            for c in range(C):
                act(J.sq[:, c, :], src[:, c, :], AF.Square, [Bsrc[c]], [J.Bsq[c]])
            ps, pb = nb()
            for c in range(C):
                mm(ps[:, 0:N], ones_bf[:, :], J.sq[:, c, :], c == 0, c == C - 1, [J.Bsq[c], B_const], [pb])
            i = J.rsi
            J.rsi = 1 - i
            rs, Brs = J.rs[i], J.Brs[i]
            ts(J.r1[:, :], ps[:, 0:N], 1.0 / (C * 128), eps, ALU.mult, ALU.add, [pb], [J.Br1])
            act(J.r1[:, :], J.r1[:, :], AF.Sqrt, [J.Br1], [J.Br1])
            P.op("dve", lambda e, a=rs[:, :], b=J.r1[:, :]: e.reciprocal(out=a, in_=b), [J.Br1], [Brs])
            return rs, Brs

        def bc(J, ap_cs):
            return ap_cs.unsqueeze(2).to_broadcast([128, J.S, J.T])

        def prenorm(J, l, kind_gs, kind_sh_base):
            S, T, N = J.S, J.T, J.N
            rs, Brs = rstd_of(J, J.x, J.Bx, 8, 1e-6)
            cols = slice(J.col0, J.col0 + S)
            for c in range(8):
                gs = mder[:, l, kind_gs, c, cols]
                sh = modT[:, l, kind_sh_base + c, cols]
                if S == 1:
                    stt(J.tmp[:, c, :], J.x[:, c, :], gs, rs[:, :], ALU.mult, ALU.mult,
                        [J.Bx[c], Brs, B_mod], [J.Btmp[c]])
                    act(J.h[:, c, :], J.tmp[:, c, :], AF.Identity, [J.Btmp[c], B_mod], [J.Bh[c]], bias=sh)
                else:
                    tt(J.tmp[:, c, :], J.x[:, c, :], rs[:, :], ALU.mult, [J.Bx[c], Brs], [J.Btmp[c]])
                    tt(v3(J.tmp[:, c, :], S, T), v3(J.tmp[:, c, :], S, T), bc(J, gs), ALU.mult,
                       [J.Btmp[c], B_mod], [J.Btmp[c]])
                    tt(v3(J.h[:, c, :], S, T), v3(J.tmp[:, c, :], S, T), bc(J, sh), ALU.add,
                       [J.Btmp[c], B_mod], [J.Bh[c]])

        def postnorm_residual(J, l, kind_gg):
            S, T, N = J.S, J.T, J.N
            rs, Brs = rstd_of(J, J.o, J.Bo, 8, 1e-6)
            cols = slice(J.col0, J.col0 + S)
            for c in range(8):
                gg = mder[:, l, kind_gg, c, cols]
                if S == 1:
                    stt(J.tmp[:, c, :], J.o[:, c, :], gg, rs[:, :], ALU.mult, ALU.mult,
                        [J.Bo[c], Brs, B_mod], [J.Btmp[c]])
                else:
                    tt(J.tmp[:, c, :], J.o[:, c, :], rs[:, :], ALU.mult, [J.Bo[c], Brs], [J.Btmp[c]])
                    tt(v3(J.tmp[:, c, :], S, T), v3(J.tmp[:, c, :], S, T), bc(J, gg), ALU.mult,
                       [J.Btmp[c], B_mod], [J.Btmp[c]])
                tt(J.x[:, c, :], J.x[:, c, :], J.tmp[:, c, :], ALU.add, [J.Btmp[c], J.Bx[c]], [J.Bx[c]])

        def proj(J, wsrc, mc_list, KC, rhs_fn, consume):
            N = J.N
            for mc in mc_list:
                wt, wb = wload(wsrc[mc], KC * 128)
                ps, pb = nb()
                for kc in range(KC):
                    r_ap, r_buf = rhs_fn(kc)
                    mm(ps[:, 0:N], wt[:, kc * 128:(kc + 1) * 128], r_ap, kc == 0, kc == KC - 1, [wb, r_buf], [pb])
                consume(mc, ps, pb)

        def hrhs(J):
            return lambda kc: (J.h[:, kc, :], J.Bh[kc])

        def conformer(J, l, j, hist_fn, tail_fn):
            S, T, N = J.S, J.T, J.N
            P.barrier()
            hist_fn(J)

            def cons_g(mc, ps, pb):
                c = mc - 8
                k = c % 2
                act(J.sig[:, k, :], ps[:, 0:N], AF.Sigmoid, [pb, B_par], [J.Bsig[k]], bias=cf_b1[:, j, mc:mc + 1])

            def cons_a(mc, ps, pb):
                c = mc
                k = c % 2
                stt(J.u[:, c, :, 30:30 + T], v3(ps[:, 0:N], S, T), cf_b1[:, j, mc:mc + 1],
                    v3(J.sig[:, k, :], S, T), ALU.add, ALU.mult, [pb, J.Bsig[k], B_par], [J.Bu[c]])

            for c in range(8):
                proj(J, I["cf_w1"], [j * 16 + 8 + c], 8, hrhs(J), lambda mc, ps, pb, j=j: cons_g(mc - j * 16, ps, pb))
                proj(J, I["cf_w1"], [j * 16 + c], 8, hrhs(J), lambda mc, ps, pb, j=j: cons_a(mc - j * 16, ps, pb))
            tail_fn(J)
            for c in range(8):
                eng = "dve"
                ov = v3(J.o[:, c, :], S, T)
                ts(ov, J.u[:, c, :, 0:T], cf_wdw[:, j, c, 0:1], cf_vec[:, j, 0, c:c + 1], ALU.mult, ALU.add,
                   [J.Bu[c], B_par], [J.Bo[c]], eng=eng)
                for k in range(1, 31):
                    stt(ov, J.u[:, c, :, k:k + T], cf_wdw[:, j, c, k:k + 1], ov, ALU.mult, ALU.add,
                        [J.Bu[c], B_par, J.Bo[c]], [J.Bo[c]], eng=eng)
            for c in range(8):
                cp(J.yb[:, c, :], J.o[:, c, :], [J.Bo[c]], [J.Byb[c]], eng="pool")
            ps, pb = nb()
            for c in range(8):
                mm(ps[:, 0:N], ones_bf[:, :], J.yb[:, c, :], c == 0, c == 7, [J.Byb[c], B_const], [pb])
            ts(J.r1[:, :], ps[:, 0:N], -1.0 / D, None, ALU.mult, None, [pb], [J.Br1])
            for c in range(8):
                tt(J.o[:, c, :], J.o[:, c, :], J.r1[:, :], ALU.add, [J.Bo[c], J.Br1], [J.Bo[c]])
            rs, Brs = rstd_of(J, J.o, J.Bo, 8, 1e-5)
            for c in range(8):
                tt(J.tmp[:, c, :], J.o[:, c, :], rs[:, :], ALU.mult, [J.Bo[c], Brs], [J.Btmp[c]])
                act(J.yb[:, c, :], J.tmp[:, c, :], AF.Silu, [J.Btmp[c], B_par], [J.Byb[c]],
                    bias=cf_vec[:, j, 2, c:c + 1], scale=cf_vec[:, j, 1, c:c + 1])

            def cons_o(mc, ps, pb):
                c = mc - j * 8
                act(J.o[:, c, :], ps[:, 0:N], AF.Identity, [pb, B_par], [J.Bo[c]], bias=cf_vec[:, j, 3, c:c + 1])

            proj(J, I["cf_w2"], [j * 8 + c for c in range(8)], 8, lambda kc: (J.yb[:, kc, :], J.Byb[kc]), cons_o)

        def shortconv(J, hist_fn, tail_fn):
            S, T, N = J.S, J.T, J.N
            P.barrier()
            hist_fn(J)

            def cons_b(mc, ps, pb):
                cp(J.bg[:, mc, :], ps[:, 0:N], [pb], [J.Bbg[mc]], eng="act_copy")

            def cons_c(mc, ps, pb):
                c = mc - 8
                cp(J.tmp[:, c, :], ps[:, 0:N], [pb], [J.Btmp[c]], eng="act_copy")

            def cons_x(mc, ps, pb):
                c = mc - 16
                tt(J.cx[:, c, :, 2:2 + T], v3(ps[:, 0:N], S, T), v3(J.tmp[:, c, :], S, T), ALU.mult,
                   [pb, J.Btmp[c]], [J.Bcx[c]])

            for c in range(8):
                proj(J, I["sc_win"], [c], 8, hrhs(J), cons_b)
                proj(J, I["sc_win"], [8 + c], 8, hrhs(J), cons_c)
                proj(J, I["sc_win"], [16 + c], 8, hrhs(J), cons_x)
            tail_fn(J)
            for c in range(8):
                ov = v3(J.o[:, c, :], S, T)
                ts(ov, J.cx[:, c, :, 0:T], sc_wc[:, c, 0:1], None, ALU.mult, None, [J.Bcx[c], B_par], [J.Bo[c]])
                for k in (1, 2):
                    stt(ov, J.cx[:, c, :, k:k + T], sc_wc[:, c, k:k + 1], ov, ALU.mult, ALU.add,
                        [J.Bcx[c], B_par, J.Bo[c]], [J.Bo[c]])
                tt(J.yb[:, c, :], J.o[:, c, :], J.bg[:, c, :], ALU.mult, [J.Bo[c], J.Bbg[c]], [J.Byb[c]])

            def cons_o(mc, ps, pb):
                cp(J.o[:, mc, :], ps[:, 0:N], [pb], [J.Bo[mc]], eng="act_copy")

            proj(J, I["sc_wout"], list(range(8)), 8, lambda kc: (J.yb[:, kc, :], J.Byb[kc]), cons_o)

        def convffn(J, l, hist_fn, tail_fn):
            S, T, N = J.S, J.T, J.N
            for i in range(22):
                sl = i % 2
                for ag in (0, 1):
                    ch = i + 22 * ag
                    hist_fn(J, l, ch, J.up[:, sl, ag, :, 0:2], J.Bup[sl][ag])

                    def cons(mc, ps, pb, sl=sl, ag=ag):
                        cp(J.up[:, sl, ag, :, 2:2 + T], v3(ps[:, 0:N], S, T), [pb], [J.Bup[sl][ag]], eng="act_copy")

                    proj(J, I["ffn_up"], [l * 44 + ch], 8, hrhs(J), cons)
                    tail_fn(J, l, ch, J.up[:, sl, ag, :, T:T + 2], J.Bup[sl][ag])
                    ov = v3(J.fa[:, sl, ag, :], S, T)
                    ts(ov, J.up[:, sl, ag, :, 0:T], ffn_wc[:, l, ch, 0:1], ffn_bc[:, l, ch:ch + 1], ALU.mult, ALU.add,
                       [J.Bup[sl][ag], B_par], [J.Bfa[sl][ag]])
                    for k in (1, 2):
                        stt(ov, J.up[:, sl, ag, :, k:k + T], ffn_wc[:, l, ch, k:k + 1], ov, ALU.mult, ALU.add,
                            [J.Bup[sl][ag], B_par, J.Bfa[sl][ag]], [J.Bfa[sl][ag]])
                act(J.fa[:, sl, 1, :], J.fa[:, sl, 1, :], AF.Silu, [J.Bfa[sl][1]], [J.Bfa[sl][1]])
                tt(J.hid[:, i, :], J.fa[:, sl, 0, :], J.fa[:, sl, 1, :], ALU.mult,
                   [J.Bfa[sl][0], J.Bfa[sl][1]], [J.Bhid[i]], eng="pool")

            def cons_o(mc, ps, pb):
                c = mc - l * 8
                cp(J.o[:, c, :], ps[:, 0:N], [pb], [J.Bo[c]], eng="act_copy")

            proj(J, I["ffn_dn"], [l * 8 + c for c in range(8)], 22, lambda kc: (J.hid[:, kc, :], J.Bhid[kc]), cons_o)

        _cp = cp

        def cp(out, in_, rd, wr, eng="dve"):
            if eng == "act_copy":
                act(out, in_, AF.Copy, rd, wr)
            else:
                _cp(out, in_, rd, wr, eng=eng)

        PO, BPO = banks[7], bbufs[7]

        def attn_proj(J, qT_dst, kT_dst, tok_consume, need_ktok=True):
            N = J.N

            def cons_q(mc, ps, pb):
                act(qT_dst(mc)[0], ps[:, 0:N], AF.Copy, [pb], [qT_dst(mc)[1]], scale=0.125)

            def cons_k(mc, ps, pb):
                act(kT_dst(mc - 8)[0], ps[:, 0:N], AF.Copy, [pb], [kT_dst(mc - 8)[1]])

            proj(J, I["sb_wqk"], list(range(8)), 8, hrhs(J), cons_q)
            proj(J, I["sb_wqk"], list(range(8, 16)), 8, hrhs(J), cons_k)
            NB = N // 128
            for half in (range(4) if need_ktok else (2, 3)):
                pbs = [nb() for _ in range(NB)]
                for kh in range(2):
                    wt, wb = wload(I["sb_wkv2"][half * 2 + kh], 2048)
                    for tb in range(NB):
                        for kcl in range(4):
                            kc = kh * 4 + kcl
                            mm(pbs[tb][0][:, :], J.h[:, kc, tb * 128:(tb + 1) * 128], wt[:, kcl * 512:(kcl + 1) * 512],
                               kc == 0, kc == 7, [J.Bh[kc], wb], [pbs[tb][1]])
                for tb in range(NB):
                    tok_consume(tb, half, pbs[tb][0], pbs[tb][1])

        class AT:
            pass

        def alloc_att(N, pref):
            A = AT()
            A.N = N
            A.e = [sb(pref + "_e%d" % i, [128, N], F32) for i in range(2)]
            A.l = [sb(pref + "_l%d" % i, [128, N], BF16) for i in range(2)]
            A.t = [sb(pref + "_t%d" % i, [128, N], F32) for i in range(2)]
            A.A = [sb(pref + "_A%d" % i, [128, N], BF16) for i in range(2)]
            A.R = sb(pref + "_R", [128, N], F32)
            A.Be = [Buf(), Buf()]
            A.Bl = [Buf(), Buf()]
            A.Bt = [Buf(), Buf()]
            A.BA = [Buf(), Buf()]
            A.BR = Buf()
            A.i = 0
            return A

        def att_block(A, qk_fn, bias_ap, bias_rd, mask_fn, av_fn, first):
            N = A.N
            i = A.i
            A.i = 1 - i
            psz, pbz = nb()
            qk_fn(psz, pbz, mask_fn is None)
            if mask_fn is not None:
                mask_fn(psz, pbz)
            act(A.e[i][:, :], psz[:, 0:N], AF.Exp, [pbz] + bias_rd, [A.Be[i]], bias=bias_ap)
            act(A.l[i][:, :], A.e[i][:, :], AF.Ln, [A.Be[i]], [A.Bl[i]], bias=1.0)
            mm(psz[:, 0:N], negU[:, :], A.l[i][:, :], False, True, [A.Bl[i], B_const], [pbz])
            psr, pbr = nb()
            mm(psr[:, 0:N], negones[:, :], A.l[i][:, :], True, True, [A.Bl[i], B_const], [pbr])
            if first:
                act(A.A[i][:, :], psz[:, 0:N], AF.Exp, [pbz] + bias_rd, [A.BA[i]], bias=bias_ap)
                cp(A.R[:, :], psr[:, 0:N], [pbr], [A.BR])
            else:
                tt(A.t[i][:, :], psz[:, 0:N], A.R[:, :], ALU.add, [pbz, A.BR], [A.Bt[i]])
                act(A.A[i][:, :], A.t[i][:, :], AF.Exp, [A.Bt[i]] + bias_rd, [A.BA[i]], bias=bias_ap)
                tt(A.R[:, :], A.R[:, :], psr[:, 0:N], ALU.add, [pbr, A.BR], [A.BR])
            av_fn(A.A[i], A.BA[i])

        vnew_s = dscr("vnew_s", [128, D], BF16)
        B_vns = Buf()

        def sample_attn_layer(J):
            with contextlib.ExitStack() as ast:
                P.barrier()
                cur[0] = ast
                sq_T = sb("sq_T", [128, 8, 128], BF16)
                Bsq_T = [Buf() for _ in range(8)]
                sk_T = sb("sk_T", [128, 8, 128], BF16)
                Bsk_T = [Buf() for _ in range(8)]
                qz = sb("sqz", [128, 16, 128], BF16)
                B_qz = Buf()
                knew = sb("knew", [128, 8, 128], BF16)
                B_knew = Buf()
                skv_tok = sb("skv_tok", [128, 2, D], F32)
                Bskv = [Buf(), Buf()]
                sv_new = sb("sv_new", [128, D], BF16)
                B_svn = Buf()
                vnew_pad = sb("vnew_pad", [128, D], BF16)
                B_vnp = Buf()
                so_T = sb("so_T", [128, 8, 128], BF16)
                Bso = [Buf() for _ in range(8)]
                pidx = sb("pidx", [128, 16 * NPG], I32)
                ptb = sb("ptb", [128, 16 * NPG], I32)
                pidxf = sb("pidxf", [128, 16 * NPG], F32)
                iot = sb("iot", [128, 1], I32)
                iotf = sb("iotf", [128, 1], F32)
                B_pidx = Buf()
                kpage = sb("kpage", [128, D], F32)
                vpage = sb("vpage", [128, D], F32)
                Bkp, Bvp = Buf(), Buf()
                kTb = sb("kTb", [128, 8, 128], BF16)
                BkTb = [Buf(), Buf()]
                vpb = sb("vpb", [128, D], BF16)
                B_vpb = Buf()
                bias_hi = sb("bias_hi", [1, 128], BF16)
                bias_lo = sb("bias_lo", [1, 128], BF16)
                brf = sb("brf", [1, 128], F32)
                brf2 = sb("brf2", [1, 128], F32)
                onesrow = sb("onesrow", [1, 128], BF16)
                B_br = Buf()
                smask = sb("smask", [128, 128], BF16)
                smaskf = sb("smaskf", [128, 16, 8], F32)
                zb = sb("zb", [128, 1], F32)
                A = alloc_att(128, "sa")

                dma("sp", ptb[:, :], I["ptab"].rearrange("s j -> (s j)").partition_broadcast(128), [], [B_pidx])
                P.op("pool", lambda e: e.iota(iot[:, :], pattern=[[0, 1]], base=0, channel_multiplier=1),
                     [], [B_pidx])
                cp(pidxf[:, :], ptb[:, :], [B_pidx], [B_pidx])
                cp(iotf[:, :], iot[:, :], [B_pidx], [B_pidx])
                ts(pidxf[:, :], pidxf[:, :], 128.0, iotf[:, 0:1], ALU.mult, ALU.add, [B_pidx], [B_pidx])
                cp(pidx[:, :], pidxf[:, :], [B_pidx], [B_pidx])
                cp(brf[0:1, :].rearrange("p (h q) -> p h q", h=16),
                   sbb[0:1, :].unsqueeze(2).to_broadcast([1, 16, 8]), [B_par], [B_br])
                cp(bias_hi[0:1, :], brf[0:1, :], [B_br], [B_br])
                cp(brf2[0:1, :], bias_hi[0:1, :], [B_br], [B_br])
                tt(brf2[0:1, :], brf[0:1, :], brf2[0:1, :], ALU.subtract, [B_br], [B_br])
                cp(bias_lo[0:1, :], brf2[0:1, :], [B_br], [B_br])
                ms(onesrow[:, :], 1.0, [B_br])
                ms(zb[:, :], 0.0, [B_br])
                ms(smaskf[:], 0.0, [B_br], eng="pool")
                P.op("pool", lambda e: e.affine_select(
                    out=smaskf[:], in_=smaskf[:], pattern=[[0, 16], [1, 8]], compare_op=ALU.is_ge, fill=NEG,
                    base=-1, channel_multiplier=-1), [B_br], [B_br])
                cp(smask[:, :], smaskf[:].rearrange("p h q -> p (h q)"), [B_br], [B_br], eng="pool")
                ms(knew[:], 0.0, [B_knew], eng="pool")
                ms(vnew_pad[:, :], 0.0, [B_vnp], eng="pool")
                ms(qz[:], 0.0, [B_qz], eng="pool")
                if KDBG_ATT < 2:
                    cur[0] = st
                    return

                def tokc(tb, half, ps, pb):
                    kv = half // 2
                    cols = slice((half % 2) * 512, (half % 2) * 512 + 512)
                    act(skv_tok[:, kv, cols], ps[:, :], AF.Copy, [pb], [Bskv[kv]])

                attn_proj(J, lambda c: (sq_T[:, c, :], Bsq_T[c]), lambda c: (sk_T[:, c, :], Bsk_T[c]), tokc)
                dma("sp", O["nks"][:, :], skv_tok[:, 0, :], [Bskv[0]], [])
                dma("sp", O["nvs"][:, :], skv_tok[:, 1, :], [Bskv[1]], [])
                cp(sv_new[:, :], skv_tok[:, 1, :], [Bskv[1]], [B_svn])
                dma("sp", vnew_s[:, :], sv_new[:, :], [B_svn], [B_vns])
                for h in range(16):
                    hb = (h % 2) * 64
                    cp(qz[hb:hb + 64, h, :], sq_T[hb:hb + 64, h // 2, :], [Bsq_T[h // 2], B_qz], [B_qz])
                if KDBG_ATT < 3:
                    cur[0] = st
                    return

                for s in range(16):
                    sc_ = slice(s * 8, (s + 1) * 8)

                    def mk_qk(kblk, Bk, s=s, sc_=sc_):
                        def qk(psz, pbz, stop):
                            for h in range(16):
                                mm(psz[:, h * 8:(h + 1) * 8], kblk[:, h // 2, :], qz[:, h, sc_],
                                   h == 0, False, Bk + [B_qz], [pbz])
                            mm(psz[:, 0:128], onesrow[0:1, :], bias_hi[0:1, :], False, False, [B_br], [pbz])
                            mm(psz[:, 0:128], onesrow[0:1, :], bias_lo[0:1, :], False, stop, [B_br], [pbz])
                        return qk

                    def mk_av(vblk, Bv, first, last):
                        def av(a_ap, Ba):
                            for h in range(16):
                                c = h // 2
                                mm(PO[:, h * 8:(h + 1) * 8], vblk[:, c * 128:(c + 1) * 128], a_ap[:, h * 8:(h + 1) * 8],
                                   first and h == 0, last and h == 15, [Bv, Ba], [BPO])
                        return av

                    def maskf(psz, pbz):
                        mm(psz[:, 0:128], ident[:, :], smask[:, :], False, True, [B_br, B_const], [pbz])

                    npg = NPG if KDBG_ATT >= 4 else 0
                    cp(knew[:, :, 0:8], sk_T[:, :, sc_], Bsk_T + [B_knew], [B_knew])
                    dma("sp", vnew_pad[0:8, :], vnew_s[s * 8:(s + 1) * 8, :], [B_vns, B_vnp], [B_vnp])
                    att_block(A, mk_qk(knew, [B_knew]), zb[:, 0:1], [B_br], maskf,
                              mk_av(vnew_pad, B_vnp, True, npg == 0), True)
                    for j in range(npg - 1, -1, -1):
                        col = s * NPG + j
                        P.dma("pool", lambda e, col=col: e.indirect_dma_start(
                            out=kpage[:, :], out_offset=None, in_=I["cache_k"][:, :],
                            in_offset=bass.IndirectOffsetOnAxis(ap=pidx[:, col:col + 1], axis=0)), [B_pidx], [Bkp])
                        P.dma("pool", lambda e, col=col: e.indirect_dma_start(
                            out=vpage[:, :], out_offset=None, in_=I["cache_v"][:, :],
                            in_offset=bass.IndirectOffsetOnAxis(ap=pidx[:, col:col + 1], axis=0)), [B_pidx], [Bvp])
                        for g in range(2):
                            ps, pb = nb()
                            for q in range(4):
                                c = g * 4 + q
                                P.op("pe", lambda e, a=ps[:, q * 128:(q + 1) * 128], b=kpage[:, c * 128:(c + 1) * 128]:
                                     e.transpose(a, b, identf[:, :]), [Bkp, B_const], [pb])
                            act(kTb[:, g * 4:(g + 1) * 4, :], ps[:, :].rearrange("p (c k) -> p c k", c=4), AF.Copy,
                                [pb], [BkTb[g]])
                        cp(vpb[:, :], vpage[:, :], [Bvp], [B_vpb], eng="pool")
                        att_block(A, mk_qk(kTb, BkTb), zb[:, 0:1], [B_br], None,
                                  mk_av(vpb, B_vpb, False, j == 0), False)
                    for h in range(16):
                        hb = (h % 2) * 64
                        act(so_T[hb:hb + 64, h // 2, sc_], PO[hb:hb + 64, h * 8:(h + 1) * 8], AF.Copy,
                            [BPO], [Bso[h // 2]])

                def cons_o(mc, ps, pb):
                    act(J.o[:, mc, :], ps[:, 0:128], AF.Copy, [pb], [J.Bo[mc]])

                proj(J, I["sb_wo"], list(range(8)), 8, lambda kc: (so_T[:, kc, :], Bso[kc]), cons_o)
                P.barrier()
            cur[0] = st

        if do_sample:
            with contextlib.ExitStack() as sst:
                P.barrier()
                cur[0] = sst
                JS = make_job("s", 16, 8, 1)
                S, T, N = 16, 8, 128
                for c in range(8):
                    dma("sp", JS.x[:, c, :], I["xsT"][c * 128:(c + 1) * 128, :], [], [JS.Bx[c]])
                for l in range(4):
                    dma("sp", JS.fft[:, l], I["ffn_state"][l], [], [JS.Bfft[l]])

                def s_ffn_hist(J, l, ch, dst, Bdst):
                    cp(dst, J.fft[:, l, ch, :, :], [J.Bfft[l]], [Bdst], eng="pool")

                def s_ffn_tail(J, l, ch, src, Bsrc):
                    cp(J.fft[:, l, ch, :, :], src, [Bsrc, J.Bfft[l]], [J.Bfft[l]], eng="pool")

                def s_cf_hist(j):
                    def f(J):
                        for c in range(8):
                            dma("sp", J.u[:, c, :, 0:30], I["cf_state"][j, :, c, :, :], [], [J.Bu[c]])
                    return f

                def s_cf_tail(j):
                    def f(J):
                        for c in range(8):
                            dma("sp", O["cfs"][j, :, c, :, :], J.u[:, c, :, T:T + 30], [J.Bu[c]], [])
                    return f

                def s_sc_hist(J):
                    for c in range(8):
                        dma("sp", J.cx[:, c, :, 0:2], I["sc_state"][:, c, :, :], [], [J.Bcx[c]])

                def s_sc_tail(J):
                    for c in range(8):
                        dma("sp", O["scs"][:, c, :, :], J.cx[:, c, :, T:T + 2], [J.Bcx[c]], [])

                for l in range(KDBG_NL):
                    kind, j = l % 3, l // 3
                    prenorm(JS, l, 0, 0)
                    if kind == 0:
                        conformer(JS, l, j, s_cf_hist(j), s_cf_tail(j))
                    elif kind == 1:
                        shortconv(JS, s_sc_hist, s_sc_tail)
                    else:
                        cur[0] = sst
                        sample_attn_layer(JS)
                        cur[0] = sst
                    postnorm_residual(JS, l, 1)
                    prenorm(JS, l, 2, 24)
                    convffn(JS, l, s_ffn_hist, s_ffn_tail)
                    dma("sp", O["ffs"][l], JS.fft[:, l], [JS.Bfft[l]], [])
                    postnorm_residual(JS, l, 3)
                for c in range(8):
                    dma("sp", O["ysT"][c * 128:(c + 1) * 128, :], JS.x[:, c, :], [JS.Bx[c]], [])
                P.barrier()
            cur[0] = st

        if do_prompt:
            W0 = NPRE - NOWN - 1
            T = NT

            def prompt_fns(J, utail, B_ut, cxtail, B_ct):
                class F:
                    wt = 0
                    last = False

                def cf_hist(J_):
                    cp(J_.u[:, :, :, 0:30], utail[:], [B_ut], J_.Bu)

                def cf_tail(j):
                    def f(J_):
                        ts(utail[:], J_.u[:, :, :, T:T + 30], tmask[:, F.wt:F.wt + 1], None, ALU.mult, None,
                           J_.Bu + [B_par, B_ut], [B_ut])
                        if F.last:
                            dma("sp", O["cfp"][j], J_.u[:, :, 0, T:T + 30], J_.Bu, [])
                    return f

                def sc_hist(J_):
                    cp(J_.cx[:, :, :, 0:2], cxtail[:], [B_ct], J_.Bcx)

                def sc_tail(J_):
                    ts(cxtail[:], J_.cx[:, :, :, T:T + 2], tmask[:, F.wt:F.wt + 1], None, ALU.mult, None,
                       J_.Bcx + [B_par, B_ct], [B_ct])
                    if F.last:
                        dma("sp", O["scp"][:, :, :], J_.cx[:, :, 0, T:T + 2], J_.Bcx, [])

                def ffn_hist(J_, l, ch, dst, Bdst):
                    cp(dst, J_.fft[:, l, ch, :, :], [J_.Bfft[l]], [Bdst], eng="pool")

                def ffn_tail(J_, l, ch, src, Bsrc):
                    ts(J_.fft[:, l, ch, :, :], src, tmask[:, F.wt:F.wt + 1], None, ALU.mult, None,
                       [Bsrc, J_.Bfft[l], B_par], [J_.Bfft[l]])

                F.cf_hist, F.cf_tail, F.sc_hist, F.sc_tail = cf_hist, cf_tail, sc_hist, sc_tail
                F.ffn_hist, F.ffn_tail = ffn_hist, ffn_tail
                return F

            def job_tails(pref):
                utail = sb(pref + "_utail", [128, 8, 1, 30], F32)
                cxtail = sb(pref + "_cxtail", [128, 8, 1, 2], F32)
                B_ut, B_ct = Buf(), Buf()
                ms(utail[:], 0.0, [B_ut], eng="pool")
                ms(cxtail[:], 0.0, [B_ct], eng="pool")
                return utail, B_ut, cxtail, B_ct

            with contextlib.ExitStack() as pst:
                P.barrier()
                cur[0] = pst
                JP = make_job("pa", 1, NT, 0)
                utail, B_ut, cxtail, B_ct = job_tails("pa")
                F = prompt_fns(JP, utail, B_ut, cxtail, B_ct)
                for l in range(4):
                    ms(JP.fft[:, l], 0.0, [JP.Bfft[l]], eng="pool")
                qst = [sb("qst%d" % i, [128, NT], BF16) for i in range(2)]
                Bqst = [Buf(), Buf()]
                kst = [sb("kst%d" % i, [128, NT], BF16) for i in range(2)]
                Bkst = [Buf(), Buf()]
                tokf = [sb("tokf%d" % i, [128, 512], F32) for i in range(2)]
                Btokf = [Buf(), Buf()]
                tokb = [sb("tokb%d" % i, [128, D], BF16) for i in range(4)]
                Btokb = [Buf() for _ in range(4)]
                cnt = {"q": 0, "k": 0, "f": 0}
                for i in range(NPRE):
                    F.wt = i
                    F.last = (i == NPRE - 1)
                    own = i > W0
                    keepq = i >= W0
                    for c in range(8):
                        dma("sp", JP.x[:, c, :], I["xpT"][c * 128:(c + 1) * 128, i * NT:(i + 1) * NT], [], [JP.Bx[c]])
                    for l in (0, 1):
                        prenorm(JP, l, 0, 0)
                        if l == 0:
                            conformer(JP, 0, 0, F.cf_hist, F.cf_tail(0))
                        else:
                            shortconv(JP, F.sc_hist, F.sc_tail)
                        postnorm_residual(JP, l, 1)
                        prenorm(JP, l, 2, 24)
                        convffn(JP, l, F.ffn_hist, F.ffn_tail)
                        if F.last:
                            dma("sp", O["ffp"][l], JP.fft[:, l, :, 0, :], [JP.Bfft[l]], [])
                        postnorm_residual(JP, l, 3)
                    if keepq:
                        for c in range(8):
                            dma("sp", x_s[c * 128:(c + 1) * 128, (i - W0) * NT:(i - W0 + 1) * NT], JP.x[:, c, :],
                                [JP.Bx[c]], [B_x])
                    prenorm(JP, 2, 0, 0)

                    def cons_qT(J, i=i, keepq=keepq):
                        N = J.N

                        def cq(mc, ps, pb):
                            if not keepq:
                                return
                            k = cnt["q"] % 2
                            cnt["q"] += 1
                            act(qst[k][:, :], ps[:, 0:N], AF.Copy, [pb], [Bqst[k]], scale=0.125)
                            dma("sp", qT_s[mc * 128:(mc + 1) * 128, (i - W0) * NT:(i - W0 + 1) * NT], qst[k][:, :],
                                [Bqst[k]], [B_qT])

                        def ck(mc, ps, pb):
                            c = mc - 8
                            k = cnt["k"] % 2
                            cnt["k"] += 1
                            act(kst[k][:, :], ps[:, 0:N], AF.Copy, [pb], [Bkst[k]])
                            dma("sp", kT_s[c * 128:(c + 1) * 128, i * NT:(i + 1) * NT], kst[k][:, :], [Bkst[k]], [B_kT])
                        return cq, ck

                    cq, ck = cons_qT(JP)
                    if keepq:
                        proj(JP, I["sb_wqk"], list(range(8)), 8, hrhs(JP), cq)
                    proj(JP, I["sb_wqk"], list(range(8, 16)), 8, hrhs(JP), ck)
                    NB = NT // 128
                    for half in (range(4) if own else (2, 3)):
                        pbs = [nb() for _ in range(NB)]
                        for kh in range(2):
                            wt, wb = wload(I["sb_wkv2"][half * 2 + kh], 2048)
                            for tb in range(NB):
                                for kcl in range(4):
                                    kc = kh * 4 + kcl
                                    mm(pbs[tb][0][:, :], JP.h[:, kc, tb * 128:(tb + 1) * 128],
                                       wt[:, kcl * 512:(kcl + 1) * 512], kc == 0, kc == 7, [JP.Bh[kc], wb], [pbs[tb][1]])
                        for tb in range(NB):
                            ps, pb = pbs[tb]
                            cols = slice((half % 2) * 512, (half % 2) * 512 + 512)
                            if own:
                                k = cnt["f"] % 2
                                cnt["f"] += 1
                                act(tokf[k][:, :], ps[:, :], AF.Copy, [pb], [Btokf[k]])
                                row0 = (i - W0 - 1) * NT + tb * 128
                                dma("sp", O["nkp" if half < 2 else "nvp"][row0:row0 + 128, cols], tokf[k][:, :],
                                    [Btokf[k]], [])
                            if half >= 2:
                                kk = tb
                                cp(tokb[kk][:, cols], ps[:, :], [pb], [Btokb[kk]])
                                if half == 3:
                                    kb = i * NB + tb
                                    dma("sp", v_s2[:, :, kb, :].rearrange("h p f -> p h f"),
                                        tokb[kk][:, :].rearrange("p (h f) -> p h f", h=8), [Btokb[kk]], [B_v])
                P.barrier()
            cur[0] = st

            with contextlib.ExitStack() as bst:
                P.barrier()
                cur[0] = bst
                kTp = sb("kTp", [128, SEQ], BF16)
                B_kTp = Buf()
                vP = sb("vP", [128, NKB, 128], BF16)
                B_vP = Buf()
                qp = sb("qp", [128, QN], BF16)
                B_qp = Buf()
                qzp = sb("qzp", [128, 2, QN], BF16)
                B_qzp = Buf()
                ob = sb("ob", [128, QN], BF16)
                B_ob = Buf()
                kbb = sb("kbb", [128, NKB], F32)
                B_kbb = Buf()
                A = alloc_att(NT, "pb")
                ms(qzp[:], 0.0, [B_qzp], eng="pool")
                for hp in range(8):
                    dma("sp", kTp[:, :], kT_s[hp * 128:(hp + 1) * 128, :], [B_kT], [B_kTp])
                    dma("sp", vP[:], v_s2[hp], [B_v], [B_vP])
                    dma("sp", qp[:, :], qT_s[hp * 128:(hp + 1) * 128, :], [B_qT], [B_qp])
                    cp(qzp[0:64, 0, :], qp[0:64, :], [B_qp, B_qzp], [B_qzp])
                    cp(qzp[64:128, 1, :], qp[64:128, :], [B_qp, B_qzp], [B_qzp])
                    for e_ in range(2):
                        h = hp * 2 + e_
                        hb = 64 * e_
                        ts(kbb[:, :], kmask[:, :], sbb[:, h:h + 1], None, ALU.add, None, [B_par, B_kbb], [B_kbb])
                        for qt in range(NOWN + 1):
                            wt_ = W0 + qt
                            qc = slice(qt * NT, (qt + 1) * NT)
                            nblk = wt_ * 4 + 4
                            for bi, kb in enumerate(range(nblk - 1, -1, -1)):
                                m = kb - wt_ * 4

                                def qk(psz, pbz, stop, kb=kb, e_=e_, qc=qc):
                                    mm(psz[:, :], kTp[:, kb * 128:(kb + 1) * 128], qzp[:, e_, qc], True, stop,
                                       [B_kTp, B_qzp], [pbz])

                                def mk(psz, pbz, m=m):
                                    mm(psz[:, :], ident[:, :], masks[:, m, :], False, True, [B_const], [pbz])

                                def av(a_ap, Ba, kb=kb, bi=bi):
                                    mm(PO[:, :], vP[:, kb, :], a_ap[:, :], bi == 0, kb == 0, [B_vP, Ba], [BPO])

                                att_block(A, qk, kbb[:, kb:kb + 1], [B_kbb], mk if m >= 0 else None, av, bi == 0)
                            act(ob[hb:hb + 64, qc], PO[hb:hb + 64, :], AF.Copy, [BPO, B_ob], [B_ob])
                    dma("sp", oT_s[hp * 128:(hp + 1) * 128, :], ob[:, :], [B_ob], [B_oT])
                P.barrier()
            cur[0] = st

            with contextlib.ExitStack() as cst:
                P.barrier()
                cur[0] = cst
                JC = make_job("pc", 1, NT, 0)
                utail, B_ut, cxtail, B_ct = job_tails("pc")
                F = prompt_fns(JC, utail, B_ut, cxtail, B_ct)
                for l in range(4):
                    ms(JC.fft[:, l], 0.0, [JC.Bfft[l]], eng="pool")
                for t in range(NOWN + 1):
                    F.wt = W0 + t
                    F.last = (t == NOWN)
                    for c in range(8):
                        dma("sp", JC.x[:, c, :], x_s[c * 128:(c + 1) * 128, t * NT:(t + 1) * NT], [B_x], [JC.Bx[c]])
                        dma("sp", JC.yb[:, c, :], oT_s[c * 128:(c + 1) * 128, t * NT:(t + 1) * NT], [B_oT], [JC.Byb[c]])

                    def cons_o(mc, ps, pb):
                        act(JC.o[:, mc, :], ps[:, 0:NT], AF.Copy, [pb], [JC.Bo[mc]])

                    proj(JC, I["sb_wo"], list(range(8)), 8, lambda kc: (JC.yb[:, kc, :], JC.Byb[kc]), cons_o)
                    postnorm_residual(JC, 2, 1)
                    prenorm(JC, 2, 2, 24)
                    convffn(JC, 2, F.ffn_hist, F.ffn_tail)
                    if F.last:
                        dma("sp", O["ffp"][2], JC.fft[:, 2, :, 0, :], [JC.Bfft[2]], [])
                    postnorm_residual(JC, 2, 3)
                    prenorm(JC, 3, 0, 0)
                    conformer(JC, 3, 1, F.cf_hist, F.cf_tail(1))
                    postnorm_residual(JC, 3, 1)
                    prenorm(JC, 3, 2, 24)
                    convffn(JC, 3, F.ffn_hist, F.ffn_tail)
                    if F.last:
                        dma("sp", O["ffp"][3], JC.fft[:, 3, :, 0, :], [JC.Bfft[3]], [])
                    postnorm_residual(JC, 3, 3)
                    if t >= 1:
                        for c in range(8):
                            dma("sp", O["ypT"][c * 128:(c + 1) * 128, (t - 1) * NT:t * NT], JC.x[:, c, :],
                                [JC.Bx[c]], [])
                P.barrier()
            cur[0] = st

        P.finish()
    return nc


def _wl(W):
    K, M = W.shape
    return np.ascontiguousarray(W.reshape(K // 128, 128, M // 128, 128).transpose(2, 1, 0, 3)).reshape(
        M // 128, 128, (K // 128) * 128)


def _vecT(v):
    sh = v.shape
    C = sh[-1] // 128
    a = v.reshape(sh[:-1] + (C, 128))
    return np.ascontiguousarray(np.moveaxis(a, -1, 0))


_CACHE = {}


def kernel(**inp):
    f32 = np.float32
    x_prompt = np.asarray(inp["x_prompt"], f32)
    BATCH, SEQ, _ = x_prompt.shape
    page_table = np.asarray(inp["page_table"], np.int32)
    NPG = page_table.shape[1]
    cache_k = np.asarray(inp["cache_k"], f32)
    cache_v = np.asarray(inp["cache_v"], f32)
    NPOOL = cache_k.shape[1]
    OWN = SEQ // 4
    NPRE = SEQ // NT
    NKB = SEQ // 128
    key = (SEQ, NPG, NPOOL)
    if key not in _CACHE:
        _CACHE[key] = build(SEQ, NPG, NPOOL, do_prompt=DO_PROMPT, do_sample=True)
    nc = _CACHE[key]

    g = lambda k: np.asarray(inp[k], f32)
    shared = {}
    shared["gvec"] = _vecT(np.stack([g("g_pre_mix"), g("g_post_mix"), g("g_pre_ffn"), g("g_post_ffn")], 0).reshape(16, D))
    shared["w_mod"] = np.concatenate([_wl(g("w_mod")[l]) for l in range(4)], 0)
    shared["b_mod"] = _vecT(g("b_mod").reshape(-1))
    shared["ffn_up"] = np.concatenate([_wl(g("ffn_w_up")[l]) for l in range(4)], 0)
    shared["ffn_wc"] = np.ascontiguousarray(_vecT(g("ffn_w_conv")).transpose(0, 1, 3, 2))
    shared["ffn_bc"] = _vecT(g("ffn_b_conv"))
    shared["ffn_dn"] = np.concatenate([_wl(g("ffn_w_down")[l]) for l in range(4)], 0)
    shared["cf_w1"] = np.concatenate([_wl(g("cf_w1")[j]) for j in range(2)], 0)
    shared["cf_b1"] = _vecT(g("cf_b1"))
    shared["cf_wdw"] = np.ascontiguousarray(_vecT(g("cf_w_dw")).transpose(0, 1, 3, 2))
    shared["cf_vec"] = _vecT(np.stack([g("cf_b_dw"), g("cf_ln_g"), g("cf_ln_b"), g("cf_b2")], 1))
    shared["cf_w2"] = np.concatenate([_wl(g("cf_w2")[j]) for j in range(2)], 0)
    shared["sc_win"] = _wl(g("sc_w_in")[0])
    shared["sc_wc"] = np.ascontiguousarray(_vecT(g("sc_w_conv")[0]).transpose(0, 2, 1))
    shared["sc_wout"] = _wl(g("sc_w_out")[0])
    wqkv = g("sb_w_qkv")[0]
    shared["sb_wqk"] = _wl(wqkv[:, 0:2 * D])
    wkv = wqkv[:, D:3 * D].reshape(2, 4, 128, 4, 512)
    shared["sb_wkv2"] = np.ascontiguousarray(wkv.transpose(3, 0, 2, 1, 4)).reshape(8, 128, 4 * 512)
    shared["sb_bias"] = np.ascontiguousarray(np.broadcast_to(g("sb_bias")[0][None, :], (128, 16)))
    shared["sb_wo"] = _wl(g("sb_w_o")[0])
    shared["cache_k"] = cache_k[0].reshape(NPOOL * 128, D)
    shared["cache_v"] = cache_v[0].reshape(NPOOL * 128, D)

    c_prompt, c_sample = g("c_prompt"), g("c_sample")
    x_sample = g("x_sample")
    st_cf, st_sc, st_ffn = g("state_cf_conv"), g("state_sc_conv"), g("state_ffn_conv")
    in_maps = []
    for c in range(8):
        b, r = c // 4, c % 4
        m = dict(shared)
        end = (r + 1) * OWN
        win = np.zeros((SEQ, D), f32)
        win[SEQ - end:] = x_prompt[b, :end]
        m["xpT"] = np.ascontiguousarray(win.T)
        ss = slice(16 * c, 16 * c + 16)
        m["xsT"] = np.ascontiguousarray(x_sample[ss].reshape(128, D).T)
        cc = np.concatenate([c_prompt[b:b + 1], c_sample[ss]], 0)
        m["cT"] = np.ascontiguousarray(cc.T.reshape(8, 128, 17).transpose(1, 0, 2))
        npad = (SEQ - end) // NT
        tm = np.ones((128, NPRE + 1), f32)
        tm[:, :npad] = 0.0
        m["tmask"] = tm
        km = np.zeros((128, NKB), f32)
        km[:, :(SEQ - end) // 128] = NEG
        m["kmask"] = km
        def stT(a):
            j_, s_, w_, ch_ = a.shape
            return np.ascontiguousarray(a.reshape(j_, s_, w_, ch_ // 128, 128).transpose(0, 4, 3, 1, 2))
        m["cf_state"] = stT(st_cf[:, ss])
        m["sc_state"] = stT(st_sc[:, ss])[0]
        m["ffn_state"] = stT(st_ffn[:, ss])
        m["ptab"] = np.ascontiguousarray(page_table[ss])
        in_maps.append(m)

    res = run_bass_kernel_spmd(nc, in_maps, core_ids=list(range(8))).results

    y_p = np.zeros((BATCH, SEQ, D), f32)
    nk_p = np.zeros((1, BATCH, SEQ, D), f32)
    nv_p = np.zeros((1, BATCH, SEQ, D), f32)
    cf_p = np.zeros((2, BATCH, 30, D), f32)
    sc_p = np.zeros((1, BATCH, 2, D), f32)
    ff_p = np.zeros((4, BATCH, 2, 2 * DFF), f32)
    y_s = np.zeros((128, 8, D), f32)
    nk_s = np.zeros((1, 128, 8, D), f32)
    nv_s = np.zeros((1, 128, 8, D), f32)
    cf_s = np.zeros((2, 128, 30, D), f32)
    sc_s = np.zeros((1, 128, 2, D), f32)
    ff_s = np.zeros((4, 128, 2, 2 * DFF), f32)

    def unT(a):
        p_, c_, s_, w_ = a.shape
        return a.transpose(2, 3, 1, 0).reshape(s_, w_, c_ * 128)

    for c in range(8):
        b, r = c // 4, c % 4
        R = res[c]
        ss = slice(16 * c, 16 * c + 16)
        tsl = slice(r * OWN, (r + 1) * OWN)
        y_p[b, tsl] = R["ypT"].T
        nk_p[0, b, tsl] = R["nkp"]
        nv_p[0, b, tsl] = R["nvp"]
        if r == 3:
            for j in range(2):
                cf_p[j, b] = unT(R["cfp"][j][:, :, None, :])[0]
            sc_p[0, b] = unT(R["scp"][:, :, None, :])[0]
            for l in range(4):
                ff_p[l, b] = unT(R["ffp"][l][:, :, None, :])[0]
        y_s[ss] = R["ysT"].T.reshape(16, 8, D)
        nk_s[0, ss] = R["nks"].reshape(16, 8, D)
        nv_s[0, ss] = R["nvs"].reshape(16, 8, D)
        for j in range(2):
            cf_s[j, ss] = unT(R["cfs"][j])
        sc_s[0, ss] = unT(R["scs"])
        for l in range(4):
            ff_s[l, ss] = unT(R["ffs"][l])
    PS = 128
    pshape = (1, BATCH, SEQ // PS, PS, NH, HD)
    return (y_p, y_s, nk_p.reshape(pshape), nv_p.reshape(pshape),
            nk_s.reshape(1, 128, 8, NH, HD), nv_s.reshape(1, 128, 8, NH, HD),
            cf_p, cf_s, sc_p, sc_s, ff_p, ff_s)


DO_PROMPT = True
import os
KDBG_ATT = int(os.environ.get('KDBG_ATT', '4'))
KDBG_NL = int(os.environ.get('KDBG_NL', '4'))
KDBG_PART = int(os.environ.get('KDBG_PART', '4'))
SKIP_ATTN = bool(int(os.environ.get('KDBG_SKIP_ATTN', '0')))
```
